# Optimizing a Trainium2 kernel written in Bass

```python
import jax, jax.numpy as jnp
from jax import lax
import numpy as np

D_MODEL = 2048
BATCH = 4
SEQ = 4096
DEPTH = 4

CHUNK = 64
N_MIXERS = 2
POOL_WINDOWS = (2, 4, 8, 16)
N_POOL_GROUPS = len(POOL_WINDOWS)
POOL_GROUP = D_MODEL // N_POOL_GROUPS
RET_HEADS = 8
RET_QK_DIM = D_MODEL // RET_HEADS
RET_V_DIM = 2 * D_MODEL // RET_HEADS
RET_QK = RET_HEADS * RET_QK_DIM
RET_V = RET_HEADS * RET_V_DIM
ROPE_BASE = 10000.0
D_FF = ((8 * D_MODEL // 3 + 255) // 256) * 256
N_EXPERTS = 8
TOP_K = 2
MOE_BLOCK = 512
LN_EPS = 1e-5
GN_EPS = 1e-6
ALPHA = (2 * DEPTH) ** 0.25
BETA = (8 * DEPTH) ** -0.25
N_EVEN = (DEPTH + 1) // 2
N_ODD = DEPTH // 2

kernel_name = "hybrid_pool_retention_moe_deepnorm"


def layer_norm(x, gain, bias):
    xf = x.astype(jnp.float32)
    mu = jnp.mean(xf, axis=-1, keepdims=True)
    var = jnp.mean(jnp.square(xf - mu), axis=-1, keepdims=True)
    y = (xf - mu) * lax.rsqrt(var + LN_EPS)
    return (y * gain.astype(jnp.float32) + bias.astype(jnp.float32)).astype(x.dtype)


def pool_mixer(x, w, scale):
    B, S, D = x.shape
    cs = jnp.cumsum(x.astype(jnp.float32), axis=1)
    t = jnp.arange(1, S + 1, dtype=jnp.float32)[:, None]
    outs = []
    for g, win in enumerate(POOL_WINDOWS):
        c = cs[..., g * POOL_GROUP:(g + 1) * POOL_GROUP]
        lag = jnp.pad(c[:, :S - win], ((0, 0), (win, 0), (0, 0)))
        outs.append((c - lag) / jnp.minimum(t, float(win)))
    pooled = jnp.concatenate(outs, axis=-1).astype(x.dtype) - x
    y = jnp.einsum('bsgc,gcd->bsgd', pooled.reshape(B, S, N_POOL_GROUPS, POOL_GROUP), w)
    return y.reshape(B, S, D) * scale


def rope(a, pos):
    half = a.shape[-1] // 2
    inv_freq = ROPE_BASE ** (-jnp.arange(half, dtype=jnp.float32) / half)
    ang = pos[:, None] * inv_freq[None, :]
    cos = jnp.cos(ang)[None, :, None, :]
    sin = jnp.sin(ang)[None, :, None, :]
    a1, a2 = a[..., :half], a[..., half:]
    return jnp.concatenate([a1 * cos - a2 * sin, a1 * sin + a2 * cos], axis=-1)


def retention(x, w_in, w_o):
    B, S, D = x.shape
    H, dk, dv = RET_HEADS, RET_QK_DIM, RET_V_DIM
    nc = S // CHUNK
    proj = x @ w_in
    q, k, v, g = jnp.split(proj, [RET_QK, 2 * RET_QK, 2 * RET_QK + RET_V], axis=-1)
    pos = jnp.arange(S, dtype=jnp.float32)
    q = rope(q.reshape(B, S, H, dk).astype(jnp.float32), pos)
    k = rope(k.reshape(B, S, H, dk).astype(jnp.float32), pos) * (dk ** -0.5)
    v = v.reshape(B, S, H, dv).astype(jnp.float32)

    def to_chunks(a):
        return a.reshape(B, nc, CHUNK, H, a.shape[-1]).transpose(1, 0, 3, 2, 4)

    log_gamma = jnp.log1p(-jnp.exp2(-5.0 - jnp.arange(H, dtype=jnp.float32)))
    idx = jnp.arange(CHUNK, dtype=jnp.float32)
    intra = jnp.exp(jnp.abs(idx[:, None] - idx[None, :]) * log_gamma[:, None, None])
    q_decay = jnp.exp((idx + 1.0)[None, :] * log_gamma[:, None])[..., None]
    k_decay = jnp.exp((CHUNK - 1.0 - idx)[None, :] * log_gamma[:, None])[..., None]
    chunk_decay = jnp.exp(CHUNK * log_gamma)[:, None, None]

    def step(state, qkv):
        qc, kc, vc = qkv
        scores = jnp.einsum('bhid,bhjd->bhij', qc, kc) * intra
        out = (jnp.einsum('bhij,bhje->bhie', scores, vc)
               + jnp.einsum('bhid,bhde->bhie', qc * q_decay, state))
        state = chunk_decay * state + jnp.einsum('bhjd,bhje->bhde', kc * k_decay, vc)
        return state, out

    state0 = jnp.zeros((B, H, dk, dv), jnp.float32)
    _, o = lax.scan(step, state0, (to_chunks(q), to_chunks(k), to_chunks(v)))
    o = o.transpose(1, 0, 3, 2, 4).reshape(B, S, H, dv)
    mu = jnp.mean(o, axis=-1, keepdims=True)
    var = jnp.mean(jnp.square(o - mu), axis=-1, keepdims=True)
    o = ((o - mu) * lax.rsqrt(var + GN_EPS)).reshape(B, S, RET_V)
    o = (jax.nn.silu(g.astype(jnp.float32)) * o).astype(x.dtype)
    return o @ w_o


def swiglu(x, w_gate, w_up, w_down):
    return (jax.nn.silu(x @ w_gate) * (x @ w_up)) @ w_down


def moe_swiglu(x, w_router, w_gate, w_up, w_down):
    B, S, D = x.shape
    xt = x.reshape(-1, D)
    T = xt.shape[0]
    n_assign = T * TOP_K
    logits = (xt @ w_router).astype(jnp.float32)
    top_val, top_idx = lax.top_k(logits, TOP_K)
    gates = jax.nn.softmax(top_val, axis=-1)
    flat_e = top_idx.reshape(-1)
    order = jnp.argsort(flat_e)
    sorted_e = flat_e[order]
    tok = order // TOP_K
    counts = jnp.bincount(flat_e, length=N_EXPERTS)
    padded = (counts + MOE_BLOCK - 1) // MOE_BLOCK * MOE_BLOCK
    pad_end = jnp.cumsum(padded)
    pad_start = pad_end - padded
    start = jnp.cumsum(counts) - counts
    dest = pad_start[sorted_e] + jnp.arange(n_assign, dtype=jnp.int32) - start[sorted_e]
    n_blocks = (n_assign + MOE_BLOCK - 1) // MOE_BLOCK + N_EXPERTS
    buf = jnp.zeros((n_blocks * MOE_BLOCK, D), x.dtype).at[dest].set(xt[tok])
    block_start = jnp.arange(n_blocks, dtype=jnp.int32) * MOE_BLOCK
    block_expert = jnp.minimum(jnp.searchsorted(pad_end, block_start, side='right'), N_EXPERTS - 1)

    def expert_block(args):
        xb, e = args
        return swiglu(xb, w_gate[e], w_up[e], w_down[e])

    yb = lax.map(expert_block, (buf.reshape(n_blocks, MOE_BLOCK, D), block_expert))
    y_sorted = yb.reshape(-1, D)[dest] * gates.reshape(-1)[order][:, None].astype(x.dtype)
    y = jnp.zeros_like(xt).at[tok].add(y_sorted)
    return y.reshape(B, S, D)


def setup_inputs(seed: int = 0) -> dict:
    key = jax.random.key(seed)
    ks = jax.random.split(key, 15)
    nrm = jax.random.normal
    f32 = jnp.float32
    x = nrm(ks[0], (BATCH, SEQ, D_MODEL), f32)
    ln_gain = 1.0 + 0.02 * nrm(ks[1], (DEPTH, 2, D_MODEL), f32)
    ln_bias = 0.02 * nrm(ks[2], (DEPTH, 2, D_MODEL), f32)
    pool_w = nrm(ks[3], (N_EVEN, N_POOL_GROUPS, POOL_GROUP, POOL_GROUP), f32) * (POOL_GROUP ** -0.5 * BETA)
    pool_scale = 1.0 + 0.1 * nrm(ks[4], (N_EVEN, D_MODEL), f32)
    col_scale = jnp.concatenate([jnp.ones((2 * RET_QK,), f32), jnp.full((RET_V,), BETA, f32),
                                 jnp.ones((RET_V,), f32)])
    ret_w_in = nrm(ks[5], (N_ODD, D_MODEL, 2 * RET_QK + 2 * RET_V), f32) * (D_MODEL ** -0.5) * col_scale
    ret_w_o = nrm(ks[6], (N_ODD, RET_V, D_MODEL), f32) * (RET_V ** -0.5 * BETA)
    ffn_w_gate = nrm(ks[7], (N_EVEN, D_MODEL, D_FF), f32) * (D_MODEL ** -0.5 * BETA)
    ffn_w_up = nrm(ks[8], (N_EVEN, D_MODEL, D_FF), f32) * (D_MODEL ** -0.5 * BETA)
    ffn_w_down = nrm(ks[9], (N_EVEN, D_FF, D_MODEL), f32) * (D_FF ** -0.5 * BETA)
    moe_w_router = nrm(ks[10], (N_ODD, D_MODEL, N_EXPERTS), f32) * (D_MODEL ** -0.5)
    moe_w_gate = nrm(ks[11], (N_ODD, N_EXPERTS, D_MODEL, D_FF), f32) * (D_MODEL ** -0.5 * BETA)
    moe_w_up = nrm(ks[12], (N_ODD, N_EXPERTS, D_MODEL, D_FF), f32) * (D_MODEL ** -0.5 * BETA)
    moe_w_down = nrm(ks[13], (N_ODD, N_EXPERTS, D_FF, D_MODEL), f32) * (D_FF ** -0.5 * BETA)
    return {"x": x, "ln_gain": ln_gain, "ln_bias": ln_bias, "pool_w": pool_w,
            "pool_scale": pool_scale, "ret_w_in": ret_w_in, "ret_w_o": ret_w_o,
            "ffn_w_gate": ffn_w_gate, "ffn_w_up": ffn_w_up, "ffn_w_down": ffn_w_down,
            "moe_w_router": moe_w_router, "moe_w_gate": moe_w_gate, "moe_w_up": moe_w_up,
            "moe_w_down": moe_w_down}


def reference(x, ln_gain, ln_bias, pool_w, pool_scale, ret_w_in, ret_w_o,
              ffn_w_gate, ffn_w_up, ffn_w_down,
              moe_w_router, moe_w_gate, moe_w_up, moe_w_down):
    for i in range(DEPTH):
        j = i // N_MIXERS
        if i % N_MIXERS == 0:
            mix = pool_mixer(x, pool_w[j], pool_scale[j])
        else:
            mix = retention(x, ret_w_in[j], ret_w_o[j])
        x = layer_norm(ALPHA * x + mix, ln_gain[i, 0], ln_bias[i, 0])
        if i % 2 == 0:
            ch = swiglu(x, ffn_w_gate[j], ffn_w_up[j], ffn_w_down[j])
        else:
            ch = moe_swiglu(x, moe_w_router[j], moe_w_gate[j], moe_w_up[j], moe_w_down[j])
        x = layer_norm(ALPHA * x + ch, ln_gain[i, 1], ln_bias[i, 1])
    return x
```

```python
import contextlib
import numpy as np
import ml_dtypes
import concourse.bass as bass
import concourse.mybir as mybir
from concourse.bass_utils import run_bass_kernel_spmd

F32 = mybir.dt.float32
BF16 = mybir.dt.bfloat16
I32 = mybir.dt.int32
AF = mybir.ActivationFunctionType
ALU = mybir.AluOpType
AX = mybir.AxisListType
ENG = ("pe", "act", "dve", "pool", "sp")
NDSEM = 8

D = 2048
NT = 4096
NTL = NT // 128
FF = 5632
NE = 8
CAP = 1408
GROUPS = (768, 640)
H = 8
DK = 256
DV = 512
RQK = 2048
RV = 4096
SEQ = 4096
ALPHA = float(8 ** 0.25)
LN_EPS = 1e-5
GN_EPS = 1e-6
POOL_WINDOWS = (2, 4, 8, 16)


class Tk:
    __slots__ = ("w", "r")

    def __init__(self):
        self.w = None
        self.r = {}


class Prog:
    def __init__(self, nc, self_sync=True):
        self.nc = nc
        self.streams = {e: [] for e in ENG}
        self.stack = contextlib.ExitStack()
        self.sems = {e: self.stack.enter_context(nc.semaphore("s_" + e)) for e in ENG}
        self.dq = {}
        for q in ("sp", "pool", "act"):
            self.dq[q] = [[self.stack.enter_context(nc.semaphore(f"d_{q}{i}")), 0] for i in range(NDSEM)]
        self.dq_rr = {q: 0 for q in self.dq}
        self.self_sync = self_sync
        self.banks = []
        self.bank_rr = 0
        self.sigcount = {e: 0 for e in ENG}
        self.waited = {e: {} for e in ENG}
        self.tks = []
        self.ninst = {e: 0 for e in ENG}
        self.uid = 0

    def tk(self):
        t = Tk()
        self.tks.append(t)
        return t

    def sb(self, name, shape, dt, ctx=None):
        self.uid += 1
        t = (ctx or self.stack).enter_context(self.nc.sbuf_tensor(f"{name}_{self.uid}", list(shape), dt))
        return t, self.tk()

    def init_psum(self):
        for i in range(8):
            t = self.stack.enter_context(self.nc.psum_tensor(f"ps{i}", [128, 512], F32))
            self.banks.append((t, self.tk()))

    def bank(self):
        b = self.banks[self.bank_rr]
        self.bank_rr = (self.bank_rr + 1) % 8
        return b

    def _deps(self, reads, writes):
        deps = []
        for t in reads:
            if t.w is not None:
                deps.append(t.w)
        for t in writes:
            if t.w is not None:
                deps.append(t.w)
            deps.extend(t.r.values())
        return deps

    def op(self, eng, fn, reads=(), writes=()):
        deps = self._deps(reads, writes)
        lst = self.streams[eng]
        ev = ("e", eng, len(lst))
        lst.append([fn, deps, False, 0])
        for t in reads:
            t.r[eng] = ev
        for t in writes:
            t.w = ev
            t.r = {}
        return ev

    def _dma(self, q, mk, reads, writes):
        slots = self.dq[q]
        j = self.dq_rr[q]
        self.dq_rr[q] = (j + 1) % len(slots)
        sem, cnt = slots[j]
        deps = self._deps(reads, writes)
        if cnt > 0:
            deps.append(("d", sem, 16 * cnt))
        slots[j][1] = cnt + 1
        ev = ("d", sem, 16 * (cnt + 1))

        def fn(e, mk=mk, sem=sem):
            return mk(e).then_inc(sem, 16)

        self.streams[q].append([fn, deps, False, 0, "DMA"])
        for t in reads:
            t.r[("d", id(sem))] = ev
        for t in writes:
            t.w = ev
            t.r = {}
        return ev

    def dma(self, q, out, in_, reads=(), writes=(), **kw):
        return self._dma(q, (lambda e, out=out, in_=in_, kw=kw: e.dma_start(out=out, in_=in_, **kw)), reads, writes)

    def scatter(self, out_dram, idx_ap, in_sb, reads=(), writes=()):
        return self._dma("pool", (lambda e: e.indirect_dma_start(
            out=out_dram, out_offset=bass.IndirectOffsetOnAxis(ap=idx_ap, axis=0), in_=in_sb, in_offset=None)), reads, writes)

    def gather(self, out_sb, in_dram, idx_ap, reads=(), writes=(), **kw):
        return self._dma("pool", (lambda e: e.indirect_dma_start(
            out=out_sb, out_offset=None, in_=in_dram, in_offset=bass.IndirectOffsetOnAxis(ap=idx_ap, axis=0), **kw)), reads, writes)

    def _last_real(self, eng, idx):
        k = idx
        lst = self.streams[eng]
        while k >= 0 and (lst[k][0] == "BAR" or len(lst[k]) > 4):
            k -= 1
        return k

    def flush(self):
        nc = self.nc
        snap_e = {e: len(self.streams[e]) - 1 for e in ENG}
        snap_d = [(s[0], 16 * s[1]) for q in self.dq for s in self.dq[q] if s[1] > 0]
        for e in ENG:
            self.streams[e].append(["BAR", snap_e, snap_d, 0])
        for eng, lst in self.streams.items():
            for rec in lst:
                if rec[0] == "BAR":
                    for e2, idx in rec[1].items():
                        if e2 != eng:
                            k = self._last_real(e2, idx)
                            if k >= 0:
                                self.streams[e2][k][2] = True
                    continue
                for d in rec[1]:
                    if d[0] == "e" and not (d[1] == eng and (eng == "pe" or not self.self_sync)):
                        self.streams[d[1]][d[2]][2] = True
        for eng, lst in self.streams.items():
            c = self.sigcount[eng]
            for rec in lst:
                if rec[0] != "BAR" and rec[2]:
                    c += 1
                    rec[3] = c
            self.sigcount[eng] = c

        def replay(eng, e):
            waited = self.waited[eng]

            def w(sem, v, key):
                if v <= 0 or waited.get(key, 0) >= v:
                    return
                waited[key] = v
                e.wait_ge(sem, v)

            for rec in self.streams[eng]:
                if rec[0] == "BAR":
                    for e2, idx in rec[1].items():
                        if e2 == eng:
                            continue
                        k = self._last_real(e2, idx)
                        if k >= 0:
                            w(self.sems[e2], self.streams[e2][k][3], e2)
                    for sem, v in rec[2]:
                        w(sem, v, id(sem))
                    continue
                for d in rec[1]:
                    if d[0] == "e":
                        if d[1] == eng and (eng == "pe" or not self.self_sync):
                            continue
                        w(self.sems[d[1]], self.streams[d[1]][d[2]][3], d[1])
                    else:
                        w(d[1], d[2], id(d[1]))
                ins = rec[0](e)
                if rec[2]:
                    ins.then_inc(self.sems[eng], 1)
                self.ninst[eng] += 1

        with nc.Block() as block:
            @block.tensor
            def _(e):
                replay("pe", e)

            @block.scalar
            def _(e):
                replay("act", e)

            @block.vector
            def _(e):
                replay("dve", e)

            @block.gpsimd
            def _(e):
                replay("pool", e)

            @block.sync
            def _(e):
                replay("sp", e)
        for e in ENG:
            self.streams[e] = []
        for t in self.tks:
            t.w = None
            t.r = {}

    def close(self):
        self.stack.close()


class Builder:
    def __init__(self, ext_in, ext_out, self_sync=True):
        self.nc = bass.Bass("TRN2", target_bir_lowering=False)
        self.ext_in = set(ext_in)
        self.ext_out = set(ext_out)
        self.drams = {}
        self.P = Prog(self.nc, self_sync)
        self.P.init_psum()
        self.consts_loaded = False

    def dram(self, name, shape, dt):
        if name in self.drams:
            return self.drams[name]
        kind = "ExternalInput" if name in self.ext_in else ("ExternalOutput" if name in self.ext_out else "Internal")
        ap = self.nc.dram_tensor(name, list(shape), dt, kind=kind).ap()
        self.drams[name] = ap
        return ap

    def load_consts(self):
        P = self.P
        self.ident_bf = P.sb("identbf", [128, 128], BF16)
        self.ident_f = P.sb("identf", [128, 128], F32)
        idb = self.dram("t_ident_bf", [128, 128], BF16)
        idf = self.dram("t_ident_f", [128, 128], F32)
        P.dma("sp", out=self.ident_bf[0][:], in_=idb[:, :], writes=[self.ident_bf[1]])
        P.dma("sp", out=self.ident_f[0][:], in_=idf[:, :], writes=[self.ident_f[1]])
        self.dtab = P.sb("dtab", [128, NTL, 2], I32)
        self.gtab = P.sb("gtab", [128, NTL, 2], F32)
        P.flush()

    def bcast_row(self, ctx, name, src_row_ap):
        P = self.P
        t = P.sb(name, [128, D], F32, ctx)
        P.dma("sp", out=t[0][:], in_=src_row_ap.partition_broadcast(128), writes=[t[1]])
        return t


def ln_stage_a(P, ts, small, eps=LN_EPS):
    (ts_t, ts_k) = ts
    (st_t, st_k), (mv_t, mv_k), (rs_t, rs_k), (nm_t, nm_k) = small
    for c in range(4):
        P.op("dve", (lambda e, o=st_t[:, c, :], i=ts_t[:, c * 512:(c + 1) * 512]: e.bn_stats(out=o, in_=i)), reads=[ts_k], writes=[st_k])
    P.op("dve", (lambda e: e.bn_aggr(out=mv_t[:], in_=st_t[:].rearrange("p a b -> p (a b)"))), reads=[st_k], writes=[mv_k])
    P.op("dve", (lambda e: e.tensor_scalar(out=rs_t[:], in0=mv_t[:, 1:2], scalar1=float(eps), scalar2=None, op0=ALU.add)),
         reads=[mv_k], writes=[rs_k])
    P.op("act", (lambda e: e.activation(out=rs_t[:], in_=rs_t[:], func=AF.Sqrt)), reads=[rs_k], writes=[rs_k])
    P.op("dve", (lambda e: e.reciprocal(out=rs_t[:], in_=rs_t[:])), reads=[rs_k], writes=[rs_k])
    P.op("dve", (lambda e: e.tensor_scalar(out=nm_t[:], in0=mv_t[:, 0:1], scalar1=rs_t[:, 0:1], scalar2=-1.0, op0=ALU.mult, op1=ALU.mult)),
         reads=[mv_k, rs_k], writes=[nm_k])


def ln_stage_b(P, ts, gain_b, xn, small):
    (ts_t, ts_k) = ts
    (st_t, st_k), (mv_t, mv_k), (rs_t, rs_k), (nm_t, nm_k) = small
    P.op("act", (lambda e: e.activation(out=xn[0][:], in_=ts_t[:], func=AF.Identity, scale=rs_t[:, 0:1], bias=nm_t[:, 0:1])),
         reads=[ts_k, rs_k, nm_k], writes=[xn[1]])
    P.op("pool", (lambda e: e.tensor_tensor(out=xn[0][:], in0=xn[0][:], in1=gain_b[0][:], op=ALU.mult)), reads=[xn[1], gain_b[1]], writes=[xn[1]])


def ln_stage_c(P, bias_b, o32, obf, xn):
    P.op("dve", (lambda e: e.tensor_tensor(out=o32[0][:], in0=xn[0][:], in1=bias_b[0][:], op=ALU.add)), reads=[xn[1], bias_b[1]], writes=[o32[1]])
    P.op("act", (lambda e: e.copy(out=obf[0][:], in_=o32[0][:])), reads=[o32[1]], writes=[obf[1]])


class LNBufs:
    def __init__(self, P, ctx, gain_b, bias_b, stq="pool"):
        self.P = P
        self.stq = stq
        self.gain_b = gain_b
        self.bias_b = bias_b
        self.o32 = [P.sb("lno32", [128, D], F32, ctx) for _ in range(3)]
        self.obf = [P.sb("lnobf", [128, D], BF16, ctx) for _ in range(3)]
        self.xn = [P.sb("lnxn", [128, D], F32, ctx) for _ in range(3)]
        self.small = [(P.sb("lnst", [128, 4, 6], F32, ctx), P.sb("lnmv", [128, 2], F32, ctx),
                       P.sb("lnrs", [128, 1], F32, ctx), P.sb("lnnm", [128, 1], F32, ctx)) for _ in range(3)]
        self.rr = 0
        self.pend = []

    def _b(self, ent):
        ts, i, dst32, dstbf = ent[:4]
        ln_stage_b(self.P, ts, self.gain_b, self.xn[i], self.small[i])
        ent[4] = 1

    def _c(self, ent):
        P = self.P
        ts, i, dst32, dstbf = ent[:4]
        ln_stage_c(P, self.bias_b, self.o32[i], self.obf[i], self.xn[i])
        P.dma(self.stq, out=dst32, in_=self.o32[i][0][:], reads=[self.o32[i][1]])
        if dstbf is not None:
            P.dma(self.stq, out=dstbf, in_=self.obf[i][0][:], reads=[self.obf[i][1]])

    def run(self, ts, dst32, dstbf):
        i = self.rr
        self.rr = (self.rr + 1) % 3
        ln_stage_a(self.P, ts, self.small[i])
        self.pend.append([ts, i, dst32, dstbf, 0])
        if len(self.pend) >= 2:
            self._b(self.pend[-2])
        if len(self.pend) >= 3:
            self._c(self.pend.pop(0))

    def drain(self):
        while self.pend:
            ent = self.pend[0]
            if ent[4] == 0:
                self._b(ent)
            self._c(self.pend.pop(0))
            if self.pend and self.pend[0][4] == 0:
                self._b(self.pend[0])


def phase_pool(B, li, x_src, halo_src, xo32, xobf):
    P = B.P
    j = li // 2
    pool_w = B.dram("pool_w", [2, 4, 512, 512], F32)
    pool_scale = B.dram("pool_scale", [2, D], F32)
    ln_gain = B.dram("ln_gain", [4, 2, D], F32)
    ln_bias = B.dram("ln_bias", [4, 2, D], F32)
    bands = B.dram("t_bands", [3, 4, 128, 128], F32)
    with contextlib.ExitStack() as ctx:
        pw = P.sb("poolw", [128, 4, 4, 512], BF16, ctx)
        for g in range(4):
            P.dma("pool", out=pw[0][:, g, :, :], in_=pool_w[j, g].rearrange("(c p) n -> p c n", p=128), writes=[pw[1]])
        bd = P.sb("bands", [128, 3, 4, 128], F32, ctx)
        for a in range(3):
            P.dma("sp", out=bd[0][:, a, :, :], in_=bands[a].rearrange("g p n -> p g n"), writes=[bd[1]])
        scale_b = B.bcast_row(ctx, "pscale", pool_scale[j:j + 1, :])
        gain_b = B.bcast_row(ctx, "gain", ln_gain[li, 0:1, :])
        bias_b = B.bcast_row(ctx, "bias", ln_bias[li, 0:1, :])
        ln = LNBufs(P, ctx, gain_b, bias_b)
        xt = [P.sb("poolx", [128, D], F32, ctx) for _ in range(3)]
        pT = [P.sb("poolpT", [128, 16, 128], BF16, ctx) for _ in range(2)]
        ts = [P.sb("poolts", [128, D], F32, ctx) for _ in range(3)]
        P.dma("sp", out=xt[2][0][:], in_=halo_src[:, :], writes=[xt[2][1]])
        prev = xt[2]
        ev = 0
        for t in range(NTL):
            cur = xt[t % 3]
            P.dma("sp", out=cur[0][:], in_=x_src[t * 128:(t + 1) * 128, :], writes=[cur[1]])
            pt_t, pt_k = pT[t % 2]
            ba = 0 if t == 0 else 1
            for g in range(4):
                bk, bkk = P.bank()
                for cc in range(4):
                    c = 4 * g + cc
                    P.op("pe", (lambda e, o=bk[:, cc * 128:(cc + 1) * 128], l=cur[0][:, c * 128:(c + 1) * 128], r=bd[0][:, ba, g, :]:
                                e.matmul(o, l, r, start=True, stop=False)), reads=[cur[1], bd[1]], writes=[bkk])
                    P.op("pe", (lambda e, o=bk[:, cc * 128:(cc + 1) * 128], l=prev[0][:, c * 128:(c + 1) * 128], r=bd[0][:, 2, g, :]:
                                e.matmul(o, l, r, start=False, stop=True)), reads=[prev[1], bd[1]], writes=[bkk])
                o = pt_t[:, 4 * g:4 * g + 4, :]
                i = bk[:, 0:512].rearrange("p (a b) -> p a b", a=4)
                if ev % 2 == 0:
                    P.op("act", (lambda e, o=o, i=i: e.copy(out=o, in_=i)), reads=[bkk], writes=[pt_k])
                else:
                    P.op("dve", (lambda e, o=o, i=i: e.tensor_copy(out=o, in_=i)), reads=[bkk], writes=[pt_k])
                ev += 1
            ts_t, ts_k = ts[t % 3]
            for g in range(4):
                bk, bkk = P.bank()
                for cc in range(4):
                    P.op("pe", (lambda e, o=bk[:, 0:512], l=pt_t[:, 4 * g + cc, :], r=pw[0][:, g, cc, :], s0=(cc == 0), s1=(cc == 3):
                                e.matmul(o, l, r, start=s0, stop=s1)), reads=[pt_k, pw[1]], writes=[bkk])
                P.op("dve", (lambda e, o=ts_t[:, g * 512:(g + 1) * 512], a=bk[:, 0:512], b=scale_b[0][:, g * 512:(g + 1) * 512]:
                             e.tensor_tensor(out=o, in0=a, in1=b, op=ALU.mult)), reads=[bkk, scale_b[1]], writes=[ts_k])
            P.op("dve", (lambda e, o=ts_t[:], a=cur[0][:]: e.scalar_tensor_tensor(out=o, in0=a, scalar=ALPHA, in1=o, op0=ALU.mult, op1=ALU.add)),
                 reads=[cur[1], ts_k], writes=[ts_k])
            ln.run((ts_t, ts_k), xo32[t * 128:(t + 1) * 128, :], xobf[t * 128:(t + 1) * 128, :])
            prev = cur
        ln.drain()
        P.flush()


def transpose_rows(P, src, srck, nst, KD, dstT, dstTk, ident, dt=BF16, col0=0):
    ei = 0
    per = 4 if dt == F32 else 4
    for t in range(nst):
        for k4 in range(KD // per):
            bank, bT = P.bank()
            pb = bank[:].bitcast(BF16) if dt == BF16 else bank[:]
            for kk in range(per):
                k = k4 * per + kk
                P.op("pe", (lambda e, o=pb[:, kk * 128:(kk + 1) * 128], i=src[:, t, k * 128:(k + 1) * 128]:
                            e.transpose(out=o, in_=i, identity=ident[0][:])), reads=[srck, ident[1]], writes=[bT])
            o = dstT[:, k4 * per:(k4 + 1) * per, col0 + t * 128:col0 + (t + 1) * 128]
            i = pb[:, 0:per * 128].rearrange("p (a b) -> p a b", a=per)
            if ei % 2 == 0:
                P.op("act", (lambda e, o=o, i=i: e.copy(out=o, in_=i)), reads=[bT], writes=[dstTk])
            else:
                P.op("dve", (lambda e, o=o, i=i: e.tensor_copy(out=o, in_=i)), reads=[bT], writes=[dstTk])
            ei += 1


class DownProj:
    def __init__(self, P, ctx, KF, subf, nslots=4):
        self.P = P
        self.KF = KF
        self.SUBF = subf
        self.ring = [P.sb("wd", [128, subf, 512], BF16, ctx) for _ in range(nslots)]
        self.rr = 0
        self.PFD = nslots - 1

    def plan(self, Wd):
        self.Wd = Wd
        self.items = [(c, s) for c in range(4) for s in range(self.KF // self.SUBF)]
        self.slots = {}
        self.nissued = 0

    def issue(self, n=1):
        for _ in range(n):
            i = self.nissued
            if i >= len(self.items):
                return
            c, s = self.items[i]
            t, tk = self.ring[self.rr]
            self.rr = (self.rr + 1) % len(self.ring)
            r0 = s * self.SUBF * 128
            self.P.dma("pool", out=t[:, :, :], in_=self.Wd[r0:r0 + self.SUBF * 128, c * 512:(c + 1) * 512].rearrange("(k p) n -> p k n", p=128), writes=[tk])
            self.slots[i] = (t, tk)
            self.nissued += 1

    def run(self, hT, hTk, nst, consume, after_panel=None):
        P = self.P
        i = 0
        nsub = self.KF // self.SUBF
        for c in range(4):
            banks = [P.bank() for _ in range(nst)]
            for s in range(nsub):
                while self.nissued < min(len(self.items), i + 1 + self.PFD):
                    self.issue()
                wt, wk = self.slots.pop(i)
                i += 1
                for fi in range(self.SUBF):
                    f = s * self.SUBF + fi
                    for st in range(nst):
                        b, bk = banks[st]
                        P.op("pe", (lambda e, o=b[:, 0:512], l=hT[:, f, st * 128:(st + 1) * 128], r=wt[:, fi, :], s0=(f == 0), s1=(f == self.KF - 1):
                                    e.matmul(o, l, r, start=s0, stop=s1)), reads=[wk, hTk], writes=[bk])
            for st in range(nst):
                consume(st, c, banks[st])
            if after_panel is not None:
                after_panel(c)


class FFN:
    def __init__(self, B, ctx, SMAX):
        P = B.P
        self.B = B
        self.P = P
        self.SMAX = SMAX
        nst = SMAX // 128
        self.xtok = P.sb("xtok", [128, nst, D], BF16, ctx)
        self.xT = P.sb("xT", [128, 16, SMAX], BF16, ctx)
        self.hT = P.sb("hT", [128, 44, SMAX], BF16, ctx)
        self.PW = 256
        big = SMAX > 640
        self.wgu = [P.sb("wgu", [128, 16, self.PW], BF16, ctx) for _ in range(4 if big else 6)]
        self.wgu_rr = 0
        self.down = DownProj(P, ctx, 44, 11, 3 if big else 4)
        self.tmp = [P.sb("silu", [128, 512], F32, ctx) for _ in range(2)]
        self.tmp_rr = 0
        self.stg = [P.sb("ystg", [128, 512], F32, ctx) for _ in range(3)]
        self.stg_rr = 0

    def _issue_gu(self, pan, Wg, Wu, p):
        P = self.P
        PW = self.PW
        if p >= FF // PW:
            return
        sl = []
        for W in (Wg, Wu):
            t, tk = self.wgu[self.wgu_rr]
            self.wgu_rr = (self.wgu_rr + 1) % len(self.wgu)
            P.dma("pool", out=t[:, :, :], in_=W[:, p * PW:(p + 1) * PW].rearrange("(k p) f -> p k f", p=128), writes=[tk])
            sl.append((t, tk))
        pan[p] = sl

    def load_rows(self, rows, S):
        xtok, xtokk = self.xtok
        self.P.dma("sp", out=xtok[:, 0:S // 128, :], in_=rows.rearrange("(t p) d -> p t d", p=128), writes=[xtokk])

    def run(self, rows, S, Wg, Wu, Wd, ydst, nxt=None):
        P = self.P
        nst = S // 128
        PW = self.PW
        xtok, xtokk = self.xtok
        xT, xTk = self.xT
        hT, hTk = self.hT
        NPAN = FF // PW
        JP = PW // 128
        PF = len(self.wgu) // 2 - 1
        if getattr(self, "pre", None) is None:
            self.load_rows(rows, S)
            pan = {}
            for p in range(PF):
                self._issue_gu(pan, Wg, Wu, p)
        else:
            pan = self.pre
            self.pre = None

        def issue_gu(p):
            self._issue_gu(pan, Wg, Wu, p)
        transpose_rows(P, xtok, xtokk, nst, 16, xT, xTk, self.B.ident_bf)
        if nxt is not None:
            self.load_rows(nxt[0], nxt[1])
        self.down.plan(Wd)
        halves = [(0, S)] if S <= 512 else [(0, S // 2), (S // 2, S // 2)]
        for p in range(NPAN):
            issue_gu(p + PF)
            if p == NPAN - 2:
                self.down.issue(3)
            (wg_t, wg_k), (wu_t, wu_k) = pan.pop(p)
            for jj in range(JP):
                j = p * JP + jj
                for (n0, n) in halves:
                    bg, bgk = P.bank()
                    bu, buk = P.bank()
                    for (wt, wk, b, bk_) in ((wg_t, wg_k, bg, bgk), (wu_t, wu_k, bu, buk)):
                        for k in range(16):
                            P.op("pe", (lambda e, o=b[:, 0:n], l=wt[:, k, jj * 128:(jj + 1) * 128], r=xT[:, k, n0:n0 + n], s0=(k == 0), s1=(k == 15):
                                        e.matmul(o, l, r, start=s0, stop=s1)), reads=[wk, xTk], writes=[bk_])
                    tm, tmk = self.tmp[self.tmp_rr]
                    self.tmp_rr ^= 1
                    P.op("act", (lambda e, o=tm[:, 0:n], i=bg[:, 0:n]: e.activation(out=o, in_=i, func=AF.Silu)), reads=[bgk], writes=[tmk])
                    P.op("dve", (lambda e, o=hT[:, j, n0:n0 + n], a=bu[:, 0:n], b_=tm[:, 0:n]: e.tensor_tensor(out=o, in0=a, in1=b_, op=ALU.mult)),
                         reads=[buk, tmk], writes=[hTk])

        def consume(st, c, bank):
            b, bk = bank
            t, tk = self.stg[self.stg_rr]
            self.stg_rr = (self.stg_rr + 1) % len(self.stg)
            P.op("act", (lambda e, o=t[:], i=b[:, 0:512]: e.copy(out=o, in_=i)), reads=[bk], writes=[tk])
            P.dma("act", out=ydst[st * 128:(st + 1) * 128, c * 512:(c + 1) * 512], in_=t[:], reads=[tk])
        if nxt is not None:
            self.pre = {}
            for p in range(PF):
                self._issue_gu(self.pre, nxt[2], nxt[3], p)
        self.down.run(hT, hTk, nst, consume)


def phase_ffn_dense(B, li, xn, ybuf):
    j = li // 2
    Wg = B.dram("ffn_w_gate", [2, D, FF], F32)
    Wu = B.dram("ffn_w_up", [2, D, FF], F32)
    Wd = B.dram("ffn_w_down", [2, FF, D], F32)
    with contextlib.ExitStack() as ctx:
        ffn = FFN(B, ctx, 512)
        ng = NT // 512
        for g in range(ng):
            nxt = (xn[(g + 1) * 512:(g + 2) * 512, :], 512, Wg[j], Wu[j]) if g + 1 < ng else None
            ffn.run(xn[g * 512:(g + 1) * 512, :], 512, Wg[j], Wu[j], Wd[j], ybuf[g * 512:(g + 1) * 512, :], nxt)
        B.P.flush()


def phase_moe_ffn(B, li, xbuf, ybuf):
    j = li // 2
    Wg = B.dram("moe_w_gate", [2, NE, D, FF], F32)
    Wu = B.dram("moe_w_up", [2, NE, D, FF], F32)
    Wd = B.dram("moe_w_down", [2, NE, FF, D], F32)
    with contextlib.ExitStack() as ctx:
        ffn = FFN(B, ctx, max(GROUPS))
        work = []
        for e in range(NE):
            r0 = e * CAP
            for gs in GROUPS:
                work.append((e, r0, gs))
                r0 += gs
        for i, (e, r0, gs) in enumerate(work):
            nxt = None
            if i + 1 < len(work):
                e2, r2, g2 = work[i + 1]
                nxt = (xbuf[r2:r2 + g2, :], g2, Wg[j, e2], Wu[j, e2])
            ffn.run(xbuf[r0:r0 + gs, :], gs, Wg[j, e], Wu[j, e], Wd[j, e], ybuf[r0:r0 + gs, :], nxt)
        B.P.flush()


def phase_ln(B, li, s, x_src, ybuf, moe, xo32, xobf):
    P = B.P
    ln_gain = B.dram("ln_gain", [4, 2, D], F32)
    ln_bias = B.dram("ln_bias", [4, 2, D], F32)
    with contextlib.ExitStack() as ctx:
        gain_b = B.bcast_row(ctx, "gain", ln_gain[li, s:s + 1, :])
        bias_b = B.bcast_row(ctx, "bias", ln_bias[li, s:s + 1, :])
        ln = LNBufs(P, ctx, gain_b, bias_b, "act" if moe else "pool")
        xt = [P.sb("lnx", [128, D], F32, ctx) for _ in range(3)]
        y1 = [P.sb("lny1", [128, D], F32, ctx) for _ in range(3)]
        y2 = [P.sb("lny2", [128, D], F32, ctx) for _ in range(3)] if moe else None
        ts = [P.sb("lnts", [128, D], F32, ctx) for _ in range(3)]
        dtab, dtk = B.dtab
        gtab, gtk = B.gtab
        for t in range(NTL):
            i = t % 3
            P.dma("sp", out=xt[i][0][:], in_=x_src[t * 128:(t + 1) * 128, :], writes=[xt[i][1]])
            ts_t, ts_k = ts[i]
            if not moe:
                P.dma("sp", out=y1[i][0][:], in_=ybuf[t * 128:(t + 1) * 128, :], writes=[y1[i][1]])
                P.op("dve", (lambda e, o=ts_t[:], a=xt[i][0][:], b=y1[i][0][:]: e.scalar_tensor_tensor(out=o, in0=a, scalar=ALPHA, in1=b, op0=ALU.mult, op1=ALU.add)),
                     reads=[xt[i][1], y1[i][1]], writes=[ts_k])
            else:
                P.gather(y1[i][0][:], ybuf[:, :], dtab[:, t, 0:1], reads=[dtk], writes=[y1[i][1]])
                P.gather(y2[i][0][:], ybuf[:, :], dtab[:, t, 1:2], reads=[dtk], writes=[y2[i][1]])
                P.op("act", (lambda e, o=ts_t[:], a=y1[i][0][:], s_=gtab[:, t, 0:1]: e.activation(out=o, in_=a, func=AF.Copy, scale=s_)),
                     reads=[y1[i][1], gtk], writes=[ts_k])
                P.op("dve", (lambda e, o=ts_t[:], a=xt[i][0][:]: e.scalar_tensor_tensor(out=o, in0=a, scalar=ALPHA, in1=o, op0=ALU.mult, op1=ALU.add)),
                     reads=[xt[i][1], ts_k], writes=[ts_k])
                P.op("dve", (lambda e, o=ts_t[:], a=y2[i][0][:], s_=gtab[:, t, 1:2]: e.scalar_tensor_tensor(out=o, in0=a, scalar=s_, in1=o, op0=ALU.mult, op1=ALU.add)),
                     reads=[y2[i][1], gtk, ts_k], writes=[ts_k])
            ln.run((ts_t, ts_k), xo32[t * 128:(t + 1) * 128, :], None if xobf is None else xobf[t * 128:(t + 1) * 128, :])
        ln.drain()
        P.flush()


def gammas():
    return [1.0 - 2.0 ** (-5.0 - h) for h in range(H)]


def phase_ret_main(B, li, xT_d, out_loc, sg, qdT_d, sfin):
    P = B.P
    j = li // 2
    w_in = B.dram("ret_w_in", [2, D, 12288], F32)
    cos_d = B.dram("t_cos", [NT, 128], F32)
    sin_d = B.dram("t_sin", [NT, 128], F32)
    qdec_d = B.dram("t_qdec", [128, H], F32)
    kdec_d = B.dram("t_kdec", [128, H], F32)
    mask_d = B.dram("t_maskT", [128, H, 128], F32)
    gam = gammas()
    with contextlib.ExitStack() as ctx:
        xTb = [P.sb("rxTb", [128, 16, 128], BF16, ctx) for _ in range(3)]
        ring = [P.sb("rw", [128, 16, 512], BF16, ctx) for _ in range(5)]
        ctab = P.sb("ctab", [128, NTL, 128], F32, ctx)
        stab = P.sb("stab", [128, NTL, 128], F32, ctx)
        qdec = P.sb("qdec", [128, H], F32, ctx)
        kdec = P.sb("kdec", [128, H], F32, ctx)
        maskT = P.sb("maskT", [128, H, 128], F32, ctx)
        P.dma("sp", out=ctab[0][:], in_=cos_d.rearrange("(t p) f -> p t f", p=128), writes=[ctab[1]])
        P.dma("sp", out=stab[0][:], in_=sin_d.rearrange("(t p) f -> p t f", p=128), writes=[stab[1]])
        P.dma("sp", out=qdec[0][:], in_=qdec_d[:, :], writes=[qdec[1]])
        P.dma("sp", out=kdec[0][:], in_=kdec_d[:, :], writes=[kdec[1]])
        P.dma("sp", out=maskT[0][:], in_=mask_d[:, :, :], writes=[maskT[1]])
        NB = 3
        qk32 = [P.sb("qk32", [128, 4, 128], F32, ctx) for _ in range(NB)]
        tA = [P.sb("ropeA", [128, 2, 128], F32, ctx) for _ in range(NB)]
        tB = [P.sb("ropeB", [128, 2, 128], F32, ctx) for _ in range(NB)]
        rr_ = [P.sb("roped", [128, 2, 2, 128], F32, ctx) for _ in range(NB)]
        qd = [P.sb("qd", [128, 256], BF16, ctx) for _ in range(NB)]
        ks = [P.sb("ks", [128, 256], BF16, ctx) for _ in range(NB)]
        kd = [P.sb("kd", [128, 256], BF16, ctx) for _ in range(NB)]
        qkT = [P.sb("qkT", [128, 4, 128], BF16, ctx) for _ in range(NB)]
        PT = [P.sb("PT", [128, 128], BF16, ctx) for _ in range(NB)]
        vbf = [P.sb("vbf", [128, 512], BF16, ctx) for _ in range(NB)]
        ostg = [P.sb("ostg", [128, 512], F32, ctx) for _ in range(NB)]
        sgt = [P.sb("sgt", [128, 512], BF16, ctx) for _ in range(NB)]
        st32 = P.sb("st32", [128, 2, 512], F32, ctx)
        stbf = [P.sb("stbf", [128, 2, 512], BF16, ctx) for _ in range(2)]

        panels = []
        for h in range(H):
            panels += [(h, "qk"), (h, "v"), (h, "g")]
        pslot = {}
        nis = [0]

        def issue_panel():
            i = nis[0]
            if i >= len(panels):
                return
            nis[0] += 1
            h, kind = panels[i]
            t, tk = ring[i % 5]
            Wj = w_in[j]
            if kind == "qk":
                P.dma("pool", out=t[:, :, 0:256], in_=Wj[:, h * 256:(h + 1) * 256].rearrange("(k p) f -> p k f", p=128), writes=[tk])
                P.dma("pool", out=t[:, :, 256:512], in_=Wj[:, RQK + h * 256:RQK + (h + 1) * 256].rearrange("(k p) f -> p k f", p=128), writes=[tk])
            else:
                c0 = (2 * RQK if kind == "v" else 2 * RQK + RV) + h * 512
                P.dma("pool", out=t[:, :, :], in_=Wj[:, c0:c0 + 512].rearrange("(k p) f -> p k f", p=128), writes=[tk])
            pslot[(h, kind)] = (t, tk)
        for _ in range(5):
            issue_panel()
        xrr = [0]

        def load_xT(b):
            t, tk = xTb[xrr[0] % 3]
            xrr[0] += 1
            P.dma("sp", out=t[:, :, :], in_=xT_d[b], writes=[tk])
            return t, tk

        def proj(wt, wk, xb_):
            xt_, xk_ = xb_
            bk, bkk = P.bank()
            for k in range(16):
                P.op("pe", (lambda e, o=bk[:, 0:512], l=xt_[:, k, :], r=wt[:, k, :], s0=(k == 0), s1=(k == 15):
                            e.matmul(o, l, r, start=s0, stop=s1)), reads=[xk_, wk], writes=[bkk])
            return bk, bkk

        it = 0
        for h in range(H):
            wqk, wqkk = pslot[(h, "qk")]
            wv, wvk = pslot[(h, "v")]
            cd = float(gam[h] ** 128)
            xq = {0: load_xT(0)}

            def stage1(b, i):
                xb_ = xq.pop(b)
                if b + 1 < NTL:
                    xq[b + 1] = load_xT(b + 1)
                bqk, bqkk = proj(wqk, wqkk, xb_)
                bv, bvk = proj(wv, wvk, xb_)
                P.op("act", (lambda e, o=vbf[i][0][:], a=bv[:, 0:512]: e.copy(out=o, in_=a)), reads=[bvk], writes=[vbf[i][1]])
                P.op("act", (lambda e, o=qk32[i][0][:].rearrange("p a f -> p (a f)"), a=bqk[:, 0:512]: e.copy(out=o, in_=a)), reads=[bqkk], writes=[qk32[i][1]])

            stage1(0, it % NB)
            for b in range(NTL):
                i = it % NB
                it += 1
                if b + 1 < NTL:
                    stage1(b + 1, it % NB)
                q32, q32k = qk32[i]
                r_, rk_ = rr_[i]
                for a in range(2):
                    a1 = q32[:, 2 * a, :]
                    a2 = q32[:, 2 * a + 1, :]
                    ta, tak = tA[i]
                    tb_, tbk = tB[i]
                    cs = ctab[0][:, b, :]
                    sn = stab[0][:, b, :]
                    P.op("dve", (lambda e, o=ta[:, 0, :], x=a1, y=cs: e.tensor_tensor(out=o, in0=x, in1=y, op=ALU.mult)), reads=[q32k, ctab[1]], writes=[tak])
                    P.op("dve", (lambda e, o=ta[:, 1, :], x=a2, y=sn: e.tensor_tensor(out=o, in0=x, in1=y, op=ALU.mult)), reads=[q32k, stab[1]], writes=[tak])
                    P.op("dve", (lambda e, o=r_[:, a, 0, :], x=ta[:, 0, :], y=ta[:, 1, :]: e.tensor_tensor(out=o, in0=x, in1=y, op=ALU.subtract)), reads=[tak], writes=[rk_])
                    P.op("dve", (lambda e, o=tb_[:, 0, :], x=a1, y=sn: e.tensor_tensor(out=o, in0=x, in1=y, op=ALU.mult)), reads=[q32k, stab[1]], writes=[tbk])
                    P.op("dve", (lambda e, o=tb_[:, 1, :], x=a2, y=cs: e.tensor_tensor(out=o, in0=x, in1=y, op=ALU.mult)), reads=[q32k, ctab[1]], writes=[tbk])
                    P.op("dve", (lambda e, o=r_[:, a, 1, :], x=tb_[:, 0, :], y=tb_[:, 1, :]: e.tensor_tensor(out=o, in0=x, in1=y, op=ALU.add)), reads=[tbk], writes=[rk_])
                rq = r_[:, 0, :, :].rearrange("p a f -> p (a f)")
                rk = r_[:, 1, :, :].rearrange("p a f -> p (a f)")
                P.op("dve", (lambda e, o=qd[i][0][:], x=rq, s_=qdec[0][:, h:h + 1]: e.tensor_scalar(out=o, in0=x, scalar1=s_, scalar2=None, op0=ALU.mult)),
                     reads=[rk_, qdec[1]], writes=[qd[i][1]])
                P.op("act", (lambda e, o=ks[i][0][:], x=rk: e.mul(out=o, in_=x, mul=0.0625)), reads=[rk_], writes=[ks[i][1]])
                P.op("dve", (lambda e, o=kd[i][0][:], x=rk, s_=kdec[0][:, h:h + 1]: e.tensor_scalar(out=o, in0=x, scalar1=s_, scalar2=None, op0=ALU.mult)),
                     reads=[rk_, kdec[1]], writes=[kd[i][1]])
                bt, btk = P.bank()
                pb = bt[:].bitcast(BF16)
                for c in range(2):
                    P.op("pe", (lambda e, o=pb[:, c * 128:(c + 1) * 128], x=qd[i][0][:, c * 128:(c + 1) * 128]: e.transpose(out=o, in_=x, identity=B.ident_bf[0][:])),
                         reads=[qd[i][1], B.ident_bf[1]], writes=[btk])
                for c in range(2):
                    P.op("pe", (lambda e, o=pb[:, (2 + c) * 128:(3 + c) * 128], x=ks[i][0][:, c * 128:(c + 1) * 128]: e.transpose(out=o, in_=x, identity=B.ident_bf[0][:])),
                         reads=[ks[i][1], B.ident_bf[1]], writes=[btk])
                qt, qtk = qkT[i]
                P.op("act", (lambda e, o=qt[:].rearrange("p a f -> p (a f)"), x=pb[:, 0:512]: e.copy(out=o, in_=x)), reads=[btk], writes=[qtk])
                if qdT_d is not None:
                    P.dma("sp", out=qdT_d[h, b], in_=qt[:, 0:2, :].rearrange("p a f -> p (a f)"), reads=[qtk])
                bs, bsk = P.bank()
                for c in range(2):
                    P.op("pe", (lambda e, o=bs[:, 0:128], l=qt[:, 2 + c, :], r=qt[:, c, :], s0=(c == 0), s1=(c == 1): e.matmul(o, l, r, start=s0, stop=s1)),
                         reads=[qtk], writes=[bsk])
                P.op("dve", (lambda e, o=PT[i][0][:], x=bs[:, 0:128], m=maskT[0][:, h, :]: e.tensor_tensor(out=o, in0=x, in1=m, op=ALU.mult)),
                     reads=[bsk, maskT[1]], writes=[PT[i][1]])
                bo, bok = P.bank()
                cur = stbf[b % 2]
                P.op("pe", (lambda e, o=bo[:, 0:512], l=PT[i][0][:], r=vbf[i][0][:], s1=(b == 0): e.matmul(o, l, r, start=True, stop=s1)),
                     reads=[PT[i][1], vbf[i][1]], writes=[bok])
                if b > 0:
                    for c in range(2):
                        P.op("pe", (lambda e, o=bo[:, 0:512], l=qt[:, c, :], r=cur[0][:, c, :], s1=(c == 1): e.matmul(o, l, r, start=False, stop=s1)),
                             reads=[qtk, cur[1]], writes=[bok])
                P.op("act", (lambda e, o=ostg[i][0][:], x=bo[:, 0:512]: e.copy(out=o, in_=x)), reads=[bok], writes=[ostg[i][1]])
                P.dma("act", out=out_loc[b * 128:(b + 1) * 128, h * 512:(h + 1) * 512], in_=ostg[i][0][:], reads=[ostg[i][1]])
                nxt = stbf[(b + 1) % 2]
                for c in range(2):
                    bst, bstk = P.bank()
                    P.op("pe", (lambda e, o=bst[:, 0:512], l=kd[i][0][:, c * 128:(c + 1) * 128], r=vbf[i][0][:]: e.matmul(o, l, r, start=True, stop=True)),
                         reads=[kd[i][1], vbf[i][1]], writes=[bstk])
                    if b == 0:
                        P.op("dve", (lambda e, o=st32[0][:, c, :], x=bst[:, 0:512]: e.tensor_copy(out=o, in_=x)), reads=[bstk], writes=[st32[1]])
                    else:
                        P.op("dve", (lambda e, o=st32[0][:, c, :], x=bst[:, 0:512], cd=cd: e.scalar_tensor_tensor(out=o, in0=o, scalar=cd, in1=x, op0=ALU.mult, op1=ALU.add)),
                             reads=[bstk, st32[1]], writes=[st32[1]])
                    if b < NTL - 1:
                        P.op("act", (lambda e, o=nxt[0][:, c, :], x=st32[0][:, c, :]: e.copy(out=o, in_=x)), reads=[st32[1]], writes=[nxt[1]])
            if sfin is not None:
                P.dma("sp", out=sfin[h].rearrange("(c p) e -> p c e", p=128), in_=st32[0][:], reads=[st32[1]])
            issue_panel()
            issue_panel()
            wg, wgk = pslot[(h, "g")]
            nxt_x = load_xT(0)
            for b in range(NTL):
                i = it % NB
                it += 1
                xb_ = nxt_x
                if b + 1 < NTL:
                    nxt_x = load_xT(b + 1)
                bg, bgk = proj(wg, wgk, xb_)
                P.op("act", (lambda e, o=sgt[i][0][:], x=bg[:, 0:512]: e.activation(out=o, in_=x, func=AF.Silu)), reads=[bgk], writes=[sgt[i][1]])
                P.dma("act", out=sg[b * 128:(b + 1) * 128, h * 512:(h + 1) * 512], in_=sgt[i][0][:], reads=[sgt[i][1]])
            issue_panel()
        P.flush()


def phase_ret_post(B, li, out_loc, sg, qdT_d, sprev, o_d):
    P = B.P
    cdec_d = B.dram("t_cdec", [128, H * NTL], F32)
    with contextlib.ExitStack() as ctx:
        if sprev is not None:
            sp_bf = P.sb("sprev", [128, H, 2, 512], BF16, ctx)
            for h in range(H):
                P.dma("pool", out=sp_bf[0][:, h, :, :], in_=sprev[h].rearrange("(c p) e -> p c e", p=128), writes=[sp_bf[1]])
        cdec = P.sb("cdec", [128, H * NTL], F32, ctx)
        if sprev is not None:
            P.dma("sp", out=cdec[0][:], in_=cdec_d[:, :], writes=[cdec[1]])
        ol = [P.sb("ol", [128, RV], F32, ctx) for _ in range(2)]
        sgx = [P.sb("sgx", [128, RV], BF16, ctx) for _ in range(2)]
        qx = [P.sb("qx", [128, H, 256], BF16, ctx) for _ in range(2)]
        ot = [P.sb("ot", [128, RV], BF16, ctx) for _ in range(2)]
        tot = [P.sb("tot", [128, 512], F32, ctx) for _ in range(2)]
        onn = [P.sb("onn", [128, 512], F32, ctx) for _ in range(2)]
        small = [(P.sb("gst", [128, 6], F32, ctx), P.sb("gmv", [128, 2], F32, ctx), P.sb("grs", [128, 1], F32, ctx), P.sb("gnm", [128, 1], F32, ctx))
                 for _ in range(2)]
        it = 0
        for b in range(NTL):
            u = b % 2
            P.dma("sp", out=ol[u][0][:], in_=out_loc[b * 128:(b + 1) * 128, :], writes=[ol[u][1]])
            P.dma("sp", out=sgx[u][0][:], in_=sg[b * 128:(b + 1) * 128, :], writes=[sgx[u][1]])
            if sprev is not None:
                P.dma("sp", out=qx[u][0][:], in_=qdT_d[:, b].rearrange("h p x -> p h x"), writes=[qx[u][1]])
            for h in range(H):
                i = it % 2
                it += 1
                if sprev is not None:
                    bk, bkk = P.bank()
                    for c in range(2):
                        P.op("pe", (lambda e, o=bk[:, 0:512], l=qx[u][0][:, h, c * 128:(c + 1) * 128], r=sp_bf[0][:, h, c, :], s0=(c == 0), s1=(c == 1):
                                    e.matmul(o, l, r, start=s0, stop=s1)), reads=[qx[u][1], sp_bf[1]], writes=[bkk])
                    tt_, ttk = tot[i]
                    tt = tt_[:]
                    P.op("dve", (lambda e, o=tt, x=bk[:, 0:512], s_=cdec[0][:, h * NTL + b:h * NTL + b + 1], y=ol[u][0][:, h * 512:(h + 1) * 512]:
                                 e.scalar_tensor_tensor(out=o, in0=x, scalar=s_, in1=y, op0=ALU.mult, op1=ALU.add)), reads=[bkk, cdec[1], ol[u][1]], writes=[ttk])
                else:
                    tt = ol[u][0][:, h * 512:(h + 1) * 512]
                    ttk = ol[u][1]
                (st_t, st_k), (mv_t, mv_k), (rs_t, rs_k), (nm_t, nm_k) = small[i]
                P.op("dve", (lambda e, o=st_t[:], x=tt: e.bn_stats(out=o, in_=x)), reads=[ttk], writes=[st_k])
                P.op("dve", (lambda e, o=mv_t[:], x=st_t[:]: e.bn_aggr(out=o, in_=x)), reads=[st_k], writes=[mv_k])
                P.op("dve", (lambda e, o=rs_t[:], x=mv_t[:, 1:2]: e.tensor_scalar(out=o, in0=x, scalar1=float(GN_EPS), scalar2=None, op0=ALU.add)), reads=[mv_k], writes=[rs_k])
                P.op("act", (lambda e, o=rs_t[:]: e.activation(out=o, in_=o, func=AF.Sqrt)), reads=[rs_k], writes=[rs_k])
                P.op("dve", (lambda e, o=rs_t[:]: e.reciprocal(out=o, in_=o)), reads=[rs_k], writes=[rs_k])
                P.op("dve", (lambda e, o=nm_t[:], x=mv_t[:, 0:1], s_=rs_t[:, 0:1]: e.tensor_scalar(out=o, in0=x, scalar1=s_, scalar2=-1.0, op0=ALU.mult, op1=ALU.mult)),
                     reads=[mv_k, rs_k], writes=[nm_k])
                on_, onk = onn[i]
                P.op("act", (lambda e, o=on_[:], x=tt, s_=rs_t[:, 0:1], b_=nm_t[:, 0:1]: e.activation(out=o, in_=x, func=AF.Identity, scale=s_, bias=b_)),
                     reads=[ttk, rs_k, nm_k], writes=[onk])
                P.op("pool", (lambda e, o=ot[u][0][:, h * 512:(h + 1) * 512], x=on_[:], y=sgx[u][0][:, h * 512:(h + 1) * 512]: e.tensor_tensor(out=o, in0=x, in1=y, op=ALU.mult)),
                     reads=[onk, sgx[u][1]], writes=[ot[u][1]])
            P.dma("pool", out=o_d[b * 128:(b + 1) * 128, :], in_=ot[u][0][:], reads=[ot[u][1]])
        P.flush()


def phase_outproj(B, li, o_d, ybuf):
    P = B.P
    j = li // 2
    w_o = B.dram("ret_w_o", [2, RV, D], F32)
    with contextlib.ExitStack() as ctx:
        otok = P.sb("otok", [128, 4, RV], BF16, ctx)
        oT = P.sb("oT", [128, 32, 512], BF16, ctx)
        down = DownProj(P, ctx, 32, 8, 4)
        stg = [P.sb("opstg", [128, 512], F32, ctx) for _ in range(3)]
        rr = [0]
        ng = NT // 512
        P.dma("sp", out=otok[0][:, :, :], in_=o_d[0:512, :].rearrange("(t p) d -> p t d", p=128), writes=[otok[1]])
        for g in range(ng):
            down.plan(w_o[j])
            down.issue(3)
            transpose_rows(P, otok[0], otok[1], 4, 32, oT[0], oT[1], B.ident_bf)
            if g + 1 < ng:
                P.dma("sp", out=otok[0][:, :, :], in_=o_d[(g + 1) * 512:(g + 2) * 512, :].rearrange("(t p) d -> p t d", p=128), writes=[otok[1]])

            def consume(st, c, bank, g=g):
                b, bk = bank
                t, tk = stg[rr[0]]
                rr[0] = (rr[0] + 1) % 3
                P.op("act", (lambda e, o=t[:], i=b[:, 0:512]: e.copy(out=o, in_=i)), reads=[bk], writes=[tk])
                P.dma("act", out=ybuf[g * 512 + st * 128:g * 512 + (st + 1) * 128, c * 512:(c + 1) * 512], in_=t[:], reads=[tk])
            down.run(oT[0], oT[1], 4, consume)
        P.flush()


def phase_route(B, li, xa, xn, xbuf):
    P = B.P
    j = li // 2
    wr_d = B.dram("moe_w_router", [2, D, NE], F32)
    triu_d = B.dram("t_triu", [128, 128], F32)
    ebase_d = B.dram("t_ebase", [128, NE], F32)
    dtab, dtk = B.dtab
    gtab, gtk = B.gtab
    with contextlib.ExitStack() as ctx:
        wr = P.sb("wr", [128, 16, NE], F32, ctx)
        P.dma("sp", out=wr[0][:], in_=wr_d[j].rearrange("(k p) e -> p k e", p=128), writes=[wr[1]])
        U = P.sb("triu", [128, 128], F32, ctx)
        P.dma("sp", out=U[0][:], in_=triu_d[:, :], writes=[U[1]])
        ones = P.sb("ones", [128, 128], F32, ctx)
        P.op("dve", (lambda e: e.memset(ones[0][:], 1.0)), writes=[ones[1]])
        ebase = P.sb("ebase", [128, NE], F32, ctx)
        P.dma("sp", out=ebase[0][:], in_=ebase_d[:, :], writes=[ebase[1]])
        carry = P.sb("carry", [128, NE], F32, ctx)
        P.op("dve", (lambda e: e.memset(carry[0][:], 0.0)), writes=[carry[1]])
        x32 = [P.sb("rx32", [128, 1, D], F32, ctx) for _ in range(2)]
        xbf = [P.sb("rxbf", [128, D], BF16, ctx) for _ in range(2)]
        zt = P.sb("zfill", [128, 8, D], BF16, ctx)
        P.op("dve", (lambda e: e.memset(zt[0][:], 0.0)), writes=[zt[1]])
        for z in range(NE * CAP // 1024):
            P.dma("sp", out=xbuf[z * 1024:(z + 1) * 1024, :].rearrange("(t p) d -> p t d", p=128), in_=zt[0][:], reads=[zt[1]])
        P.flush()
        xT32 = [P.sb("rxT32", [128, 16, 128], F32, ctx) for _ in range(2)]
        sm = [dict(lg=P.sb("lg", [128, NE], F32, ctx), m8=P.sb("m8", [128, 8], F32, ctx), sel=P.sb("sel", [128, NE], F32, ctx),
                   oh1=P.sb("oh1", [128, NE], F32, ctx), oh2=P.sb("oh2", [128, NE], F32, ctx), slot=P.sb("slot", [128, NE], F32, ctx),
                   tmp=P.sb("rtmp", [128, NE], F32, ctx), df=P.sb("df", [128, 2], F32, ctx), ex=P.sb("ex", [128, 2], F32, ctx)) for _ in range(2)]
        for t in range(NTL):
            i = t % 2
            P.dma("sp", out=x32[i][0][:, 0, :], in_=xa[t * 128:(t + 1) * 128, :], writes=[x32[i][1]])
            P.dma("sp", out=xbf[i][0][:], in_=xn[t * 128:(t + 1) * 128, :], writes=[xbf[i][1]])
            transpose_rows(P, x32[i][0], x32[i][1], 1, 16, xT32[i][0], xT32[i][1], B.ident_f, dt=F32)
            bk, bkk = P.bank()
            for k in range(16):
                P.op("pe", (lambda e, o=bk[:, 0:NE], l=xT32[i][0][:, k, :], r=wr[0][:, k, :], s0=(k == 0), s1=(k == 15): e.matmul(o, l, r, start=s0, stop=s1)),
                     reads=[xT32[i][1], wr[1]], writes=[bkk])
            d = sm[i]
            lg, m8, sel, oh1, oh2, slot, tmp, df, ex = (d[k_] for k_ in ("lg", "m8", "sel", "oh1", "oh2", "slot", "tmp", "df", "ex"))
            P.op("dve", (lambda e, o=lg[0][:], x=bk[:, 0:NE]: e.tensor_copy(out=o, in_=x)), reads=[bkk], writes=[lg[1]])
            P.op("dve", (lambda e, o=m8[0][:], x=lg[0][:]: e.max(out=o, in_=x)), reads=[lg[1]], writes=[m8[1]])
            P.op("dve", (lambda e, o=sel[0][:], x=lg[0][:], s_=m8[0][:, 1:2]: e.tensor_scalar(out=o, in0=x, scalar1=s_, scalar2=None, op0=ALU.is_ge)),
                 reads=[lg[1], m8[1]], writes=[sel[1]])
            P.op("dve", (lambda e, o=oh1[0][:], x=lg[0][:], s_=m8[0][:, 0:1]: e.tensor_scalar(out=o, in0=x, scalar1=s_, scalar2=None, op0=ALU.is_equal)),
                 reads=[lg[1], m8[1]], writes=[oh1[1]])
            P.op("dve", (lambda e, o=oh2[0][:], x=sel[0][:], y=oh1[0][:]: e.tensor_tensor(out=o, in0=x, in1=y, op=ALU.subtract)),
                 reads=[sel[1], oh1[1]], writes=[oh2[1]])
            b2, b2k = P.bank()
            P.op("pe", (lambda e, o=b2[:, 0:NE], l=U[0][:], r=sel[0][:]: e.matmul(o, l, r, start=True, stop=True)), reads=[U[1], sel[1]], writes=[b2k])
            P.op("pe", (lambda e, o=b2[:, NE:2 * NE], l=ones[0][:], r=sel[0][:]: e.matmul(o, l, r, start=True, stop=True)), reads=[ones[1], sel[1]], writes=[b2k])
            P.op("dve", (lambda e, o=slot[0][:], x=b2[:, 0:NE], y=carry[0][:]: e.tensor_tensor(out=o, in0=x, in1=y, op=ALU.add)),
                 reads=[b2k, carry[1]], writes=[slot[1]])
            P.op("dve", (lambda e, o=slot[0][:], y=ebase[0][:]: e.tensor_tensor(out=o, in0=o, in1=y, op=ALU.add)), reads=[slot[1], ebase[1]], writes=[slot[1]])
            P.op("dve", (lambda e, o=carry[0][:], x=b2[:, NE:2 * NE]: e.tensor_tensor(out=o, in0=x, in1=o, op=ALU.add)), reads=[b2k, carry[1]], writes=[carry[1]])
            for kk, oh in enumerate((oh1, oh2)):
                P.op("dve", (lambda e, o=tmp[0][:], x=oh[0][:], y=slot[0][:]: e.tensor_tensor(out=o, in0=x, in1=y, op=ALU.mult)),
                     reads=[oh[1], slot[1]], writes=[tmp[1]])
                P.op("dve", (lambda e, o=df[0][:, kk:kk + 1], x=tmp[0][:]: e.reduce_sum(out=o, in_=x, axis=AX.X)), reads=[tmp[1]], writes=[df[1]])
            P.op("dve", (lambda e, o=dtab[:, t, :], x=df[0][:]: e.tensor_copy(out=o, in_=x)), reads=[df[1]], writes=[dtk])
            P.op("dve", (lambda e, o=ex[0][:, 0:1], x=m8[0][:, 1:2], y=m8[0][:, 0:1]: e.tensor_tensor(out=o, in0=x, in1=y, op=ALU.subtract)),
                 reads=[m8[1]], writes=[ex[1]])
            P.op("act", (lambda e, o=ex[0][:, 0:1]: e.activation(out=o, in_=o, func=AF.Exp)), reads=[ex[1]], writes=[ex[1]])
            P.op("dve", (lambda e, o=ex[0][:, 1:2], x=ex[0][:, 0:1]: e.tensor_scalar(out=o, in0=x, scalar1=1.0, scalar2=None, op0=ALU.add)),
                 reads=[ex[1]], writes=[ex[1]])
            P.op("dve", (lambda e, o=gtab[:, t, 0:1], x=ex[0][:, 1:2]: e.reciprocal(out=o, in_=x)), reads=[ex[1]], writes=[gtk])
            P.op("dve", (lambda e, o=gtab[:, t, 1:2], x=ex[0][:, 0:1], y=gtab[:, t, 0:1]: e.tensor_tensor(out=o, in0=x, in1=y, op=ALU.mult)),
                 reads=[ex[1], gtk], writes=[gtk])
            for kk in range(2):
                P.scatter(xbuf[:, :], dtab[:, t, kk:kk + 1], xbf[i][0][:], reads=[dtk, xbf[i][1]])
        P.flush()


def phase_retout(B, li, out_loc, sg, o_d, ybuf, x_src, xo32, xobf):
    P = B.P
    j = li // 2
    w_o = B.dram("ret_w_o", [2, RV, D], F32)
    ln_gain = B.dram("ln_gain", [4, 2, D], F32)
    ln_bias = B.dram("ln_bias", [4, 2, D], F32)
    ng = NT // 512
    with contextlib.ExitStack() as ctx:
        otok = P.sb("otok", [128, 4, RV], BF16, ctx)
        oT = P.sb("oT", [128, 32, 512], BF16, ctx)
        down = DownProj(P, ctx, 32, 8, 4)
        stg = [P.sb("opstg", [128, 512], F32, ctx) for _ in range(3)]
        rr = [0]
        ol = [P.sb("ol", [128, RV], F32, ctx) for _ in range(2)]
        sgx = [P.sb("sgx", [128, RV], BF16, ctx) for _ in range(2)]
        ot = [P.sb("ot", [128, RV], BF16, ctx) for _ in range(2)]
        onn = [P.sb("onn", [128, 512], F32, ctx) for _ in range(2)]
        small = [(P.sb("gst", [128, 6], F32, ctx), P.sb("gmv", [128, 2], F32, ctx), P.sb("grs", [128, 1], F32, ctx), P.sb("gnm", [128, 1], F32, ctx))
                 for _ in range(2)]
        gain_b = B.bcast_row(ctx, "gain", ln_gain[li, 0:1, :])
        bias_b = B.bcast_row(ctx, "bias", ln_bias[li, 0:1, :])
        xt = P.sb("lx", [128, D], F32, ctx)
        yt = P.sb("ly", [128, D], F32, ctx)
        obf = P.sb("lobf", [128, D], BF16, ctx)
        lsm = (P.sb("lst", [128, 4, 6], F32, ctx), P.sb("lmv", [128, 2], F32, ctx), P.sb("lrs", [128, 1], F32, ctx), P.sb("lnm", [128, 1], F32, ctx))
        otk = [P.tk() for _ in range(ng)]
        ytk = [P.tk() for _ in range(ng)]
        it = [0]

        def post_tile(b):
            u = b % 2
            P.dma("sp", out=ol[u][0][:], in_=out_loc[b * 128:(b + 1) * 128, :], writes=[ol[u][1]])
            P.dma("sp", out=sgx[u][0][:], in_=sg[b * 128:(b + 1) * 128, :], writes=[sgx[u][1]])
            for h in range(H):
                i = it[0] % 2
                it[0] += 1
                tt = ol[u][0][:, h * 512:(h + 1) * 512]
                ttk = ol[u][1]
                (st_t, st_k), (mv_t, mv_k), (rs_t, rs_k), (nm_t, nm_k) = small[i]
                P.op("dve", (lambda e, o=st_t[:], x=tt: e.bn_stats(out=o, in_=x)), reads=[ttk], writes=[st_k])
                P.op("dve", (lambda e, o=mv_t[:], x=st_t[:]: e.bn_aggr(out=o, in_=x)), reads=[st_k], writes=[mv_k])
                P.op("dve", (lambda e, o=rs_t[:], x=mv_t[:, 1:2]: e.tensor_scalar(out=o, in0=x, scalar1=float(GN_EPS), scalar2=None, op0=ALU.add)), reads=[mv_k], writes=[rs_k])
                P.op("act", (lambda e, o=rs_t[:]: e.activation(out=o, in_=o, func=AF.Sqrt)), reads=[rs_k], writes=[rs_k])
                P.op("dve", (lambda e, o=rs_t[:]: e.reciprocal(out=o, in_=o)), reads=[rs_k], writes=[rs_k])
                P.op("dve", (lambda e, o=nm_t[:], x=mv_t[:, 0:1], s_=rs_t[:, 0:1]: e.tensor_scalar(out=o, in0=x, scalar1=s_, scalar2=-1.0, op0=ALU.mult, op1=ALU.mult)),
                     reads=[mv_k, rs_k], writes=[nm_k])
                on_, onk = onn[i]
                P.op("act", (lambda e, o=on_[:], x=tt, s_=rs_t[:, 0:1], b_=nm_t[:, 0:1]: e.activation(out=o, in_=x, func=AF.Identity, scale=s_, bias=b_)),
                     reads=[ttk, rs_k, nm_k], writes=[onk])
                P.op("dve", (lambda e, o=ot[u][0][:, h * 512:(h + 1) * 512], x=on_[:], y=sgx[u][0][:, h * 512:(h + 1) * 512]: e.tensor_tensor(out=o, in0=x, in1=y, op=ALU.mult)),
                     reads=[onk, sgx[u][1]], writes=[ot[u][1]])
            P.dma("act", out=o_d[b * 128:(b + 1) * 128, :], in_=ot[u][0][:], reads=[ot[u][1]], writes=[otk[b // 4]])

        def ln_one(t):
            g = t // 4
            P.dma("sp", out=xt[0][:], in_=x_src[t * 128:(t + 1) * 128, :], writes=[xt[1]])
            P.dma("sp", out=yt[0][:], in_=ybuf[t * 128:(t + 1) * 128, :], reads=[ytk[g]], writes=[yt[1]])
            P.op("dve", (lambda e, o=yt[0][:], a=xt[0][:]: e.scalar_tensor_tensor(out=o, in0=a, scalar=ALPHA, in1=o, op0=ALU.mult, op1=ALU.add)),
                 reads=[xt[1], yt[1]], writes=[yt[1]])
            (st_t, st_k), (mv_t, mv_k), (rs_t, rs_k), (nm_t, nm_k) = lsm
            for c in range(4):
                P.op("dve", (lambda e, o=st_t[:, c, :], i=yt[0][:, c * 512:(c + 1) * 512]: e.bn_stats(out=o, in_=i)), reads=[yt[1]], writes=[st_k])
            P.op("dve", (lambda e: e.bn_aggr(out=mv_t[:], in_=st_t[:].rearrange("p a b -> p (a b)"))), reads=[st_k], writes=[mv_k])
            P.op("dve", (lambda e: e.tensor_scalar(out=rs_t[:], in0=mv_t[:, 1:2], scalar1=float(LN_EPS), scalar2=None, op0=ALU.add)), reads=[mv_k], writes=[rs_k])
            P.op("act", (lambda e: e.activation(out=rs_t[:], in_=rs_t[:], func=AF.Sqrt)), reads=[rs_k], writes=[rs_k])
            P.op("dve", (lambda e: e.reciprocal(out=rs_t[:], in_=rs_t[:])), reads=[rs_k], writes=[rs_k])
            P.op("dve", (lambda e: e.tensor_scalar(out=nm_t[:], in0=mv_t[:, 0:1], scalar1=rs_t[:, 0:1], scalar2=-1.0, op0=ALU.mult, op1=ALU.mult)),
                 reads=[mv_k, rs_k], writes=[nm_k])
            P.op("act", (lambda e: e.activation(out=xt[0][:], in_=yt[0][:], func=AF.Identity, scale=rs_t[:, 0:1], bias=nm_t[:, 0:1])),
                 reads=[yt[1], rs_k, nm_k], writes=[xt[1]])
            P.op("dve", (lambda e: e.tensor_tensor(out=xt[0][:], in0=xt[0][:], in1=gain_b[0][:], op=ALU.mult)), reads=[xt[1], gain_b[1]], writes=[xt[1]])
            P.op("dve", (lambda e: e.tensor_tensor(out=yt[0][:], in0=xt[0][:], in1=bias_b[0][:], op=ALU.add)), reads=[xt[1], bias_b[1]], writes=[yt[1]])
            P.op("act", (lambda e: e.copy(out=obf[0][:], in_=yt[0][:])), reads=[yt[1]], writes=[obf[1]])
            P.dma("act", out=xo32[t * 128:(t + 1) * 128, :], in_=yt[0][:], reads=[yt[1]])
            P.dma("act", out=xobf[t * 128:(t + 1) * 128, :], in_=obf[0][:], reads=[obf[1]])

        for b in range(8):
            post_tile(b)
        P.dma("sp", out=otok[0][:, :, :], in_=o_d[0:512, :].rearrange("(t p) d -> p t d", p=128), reads=[otk[0]], writes=[otok[1]])
        for g in range(ng):
            down.plan(w_o[j])
            down.issue(3)
            transpose_rows(P, otok[0], otok[1], 4, 32, oT[0], oT[1], B.ident_bf)
            if g + 1 < ng:
                P.dma("sp", out=otok[0][:, :, :], in_=o_d[(g + 1) * 512:(g + 2) * 512, :].rearrange("(t p) d -> p t d", p=128), reads=[otk[g + 1]], writes=[otok[1]])

            def consume(st, c, bank, g=g):
                b, bk = bank
                t, tk = stg[rr[0]]
                rr[0] = (rr[0] + 1) % 3
                P.op("act", (lambda e, o=t[:], i=b[:, 0:512]: e.copy(out=o, in_=i)), reads=[bk], writes=[tk])
                P.dma("act", out=ybuf[g * 512 + st * 128:g * 512 + (st + 1) * 128, c * 512:(c + 1) * 512], in_=t[:], reads=[tk], writes=[ytk[g]])

            def after_panel(c, g=g):
                if g + 2 < ng:
                    post_tile(4 * (g + 2) + c)
                if g >= 1:
                    ln_one(4 * (g - 1) + c)
            down.run(oT[0], oT[1], 4, consume, after_panel)
        for c in range(4):
            ln_one(4 * (ng - 1) + c)
        P.flush()


def phase_build_xT(B, xn, xT_d):
    P = B.P
    with contextlib.ExitStack() as ctx:
        xtok = [P.sb("bxtok", [128, 1, D], BF16, ctx) for _ in range(2)]
        xTt = [P.sb("bxT", [128, 16, 128], BF16, ctx) for _ in range(2)]
        for t in range(NTL):
            xt_, xk_ = xtok[t % 2]
            o_, ok_ = xTt[t % 2]
            P.dma("sp", out=xt_[:, 0, :], in_=xn[t * 128:(t + 1) * 128, :], writes=[xk_])
            transpose_rows(P, xt_, xk_, 1, 16, o_, ok_, B.ident_bf)
            P.dma("pool", out=xT_d[t], in_=o_[:, :, :], reads=[ok_])
        P.flush()


W_SHAPES = {
    "ln_gain": [4, 2, D], "ln_bias": [4, 2, D], "pool_w": [2, 4, 512, 512], "pool_scale": [2, D],
    "ret_w_in": [2, D, 12288], "ret_w_o": [2, RV, D], "ffn_w_gate": [2, D, FF], "ffn_w_up": [2, D, FF],
    "ffn_w_down": [2, FF, D], "moe_w_router": [2, D, NE], "moe_w_gate": [2, NE, D, FF],
    "moe_w_up": [2, NE, D, FF], "moe_w_down": [2, NE, FF, D],
}
T_NAMES = ["t_ident_bf", "t_ident_f", "t_bands", "t_cos", "t_sin", "t_qdec", "t_kdec", "t_maskT", "t_triu", "t_ebase"]


def build_program(self_sync=True, layers=(0, 1, 2, 3)):
    ext_in = ["x", "xhalo"] + list(W_SHAPES) + T_NAMES
    B = Builder(ext_in, ["out"], self_sync)
    x = B.dram("x", [NT, D], F32)
    xhalo = B.dram("xhalo", [128, D], F32)
    out = B.dram("out", [NT, D], F32)
    for k, shp in W_SHAPES.items():
        B.dram(k, shp, F32)
    xa = B.dram("xa", [NT, D], F32)
    xb = B.dram("xb", [NT, D], F32)
    xn = B.dram("xn", [NT, D], BF16)
    ybuf_d = B.dram("ybuf_d", [NT, D], F32)
    out_loc = B.dram("out_loc", [NT, RV], F32)
    sg = B.dram("sg", [NT, RV], BF16)
    o_d = B.dram("o_d", [NT, RV], BF16)
    xT_d = B.dram("xT_d", [NTL, 128, 16, 128], BF16)
    xbuf = B.dram("xbuf", [NE * CAP, D], BF16)
    ybuf_m = B.dram("ybuf_m", [NE * CAP, D], F32)
    B.load_consts()
    src = x
    last = layers[-1]
    for li in layers:
        if li % 2 == 0:
            phase_pool(B, li, src, xhalo, xa, xn)
            phase_ffn_dense(B, li, xn, ybuf_d)
            dst = out if li == last else xb
            phase_ln(B, li, 1, xa, ybuf_d, False, dst, None if li == last else xn)
            src = xb
        else:
            if li == layers[0]:
                raise ValueError("odd first layer unsupported")
            phase_build_xT(B, xn, xT_d)
            phase_ret_main(B, li, xT_d, out_loc, sg, None, None)
            phase_ret_post(B, li, out_loc, sg, None, None, o_d)
            phase_outproj(B, li, o_d, ybuf_d)
            phase_ln(B, li, 0, src, ybuf_d, False, xa, xn)
            phase_route(B, li, xa, xn, xbuf)
            phase_moe_ffn(B, li, xbuf, ybuf_m)
            dst = out if li == last else xb
            phase_ln(B, li, 1, xa, ybuf_m, True, dst, None if li == last else xn)
            src = xb
    B.P.close()
    return B


_PROG = {}


def kernel(**inputs):
    if "nc" not in _PROG:
        _PROG["nc"] = build_program().nc
    nc = _PROG["nc"]
    tb = make_tables(0)
    x = np.ascontiguousarray(np.asarray(inputs["x"], dtype=np.float32))
    nb = x.shape[0]
    shared = {k: np.ascontiguousarray(np.asarray(inputs[k], dtype=np.float32)) for k in W_SHAPES}
    for k in T_NAMES:
        shared[k] = tb[k]
    shared["xhalo"] = np.zeros((128, D), np.float32)
    in_maps = []
    for c in range(nb):
        m = dict(shared)
        m["x"] = x[c]
        in_maps.append(m)
    res = run_bass_kernel_spmd(nc, in_maps, core_ids=list(range(nb)))
    return np.stack([np.asarray(res.results[c]["out"], dtype=np.float32) for c in range(nb)], axis=0)


def make_tables(core):
    pos0 = 0
    tb = {}
    tb["t_ident_bf"] = np.eye(128, dtype=np.float32).astype(ml_dtypes.bfloat16)
    tb["t_ident_f"] = np.eye(128, dtype=np.float32)
    bands = np.zeros((3, 4, 128, 128), np.float32)
    tp = np.arange(128)[:, None]
    tok = np.arange(128)[None, :]
    for g, w in enumerate(POOL_WINDOWS):
        inwin = (tp <= tok) & (tp > tok - w)
        cnt_first = np.minimum(tok + 1 + pos0, w).astype(np.float32)
        bands[0, g] = inwin / cnt_first - (tp == tok)
        bands[1, g] = inwin / np.float32(w) - (tp == tok)
        bands[2, g] = ((tp - 128) > (tok - w)) / np.float32(w)
    tb["t_bands"] = bands
    posv = (pos0 + np.arange(NT)).astype(np.float32)
    inv_freq = (np.float32(10000.0) ** (-np.arange(128, dtype=np.float32) / np.float32(128))).astype(np.float32)
    ang = (posv[:, None] * inv_freq[None, :]).astype(np.float32)
    tb["t_cos"] = np.cos(ang).astype(np.float32)
    tb["t_sin"] = np.sin(ang).astype(np.float32)
    gam = np.array([1.0 - 2.0 ** (-5.0 - h) for h in range(H)], np.float64)
    ii = np.arange(128, dtype=np.float64)
    tb["t_qdec"] = (gam[None, :] ** (ii[:, None] + 1.0)).astype(np.float32)
    tb["t_kdec"] = (gam[None, :] ** (127.0 - ii[:, None]) / 16.0).astype(np.float32)
    mi = ii[None, :]
    mj = ii[:, None]
    allowed = (np.floor(mj / 64) <= np.floor(mi / 64))
    maskT = np.zeros((128, H, 128), np.float64)
    for h in range(H):
        maskT[:, h, :] = allowed * gam[h] ** (np.abs(mi - mj) - (mi + 1.0))
    tb["t_maskT"] = maskT.astype(np.float32)
    cdec = np.zeros((128, H * NTL), np.float64)
    for h in range(H):
        for b in range(NTL):
            cdec[:, h * NTL + b] = gam[h] ** (128.0 * b)
    tb["t_cdec"] = cdec.astype(np.float32)
    tb["t_triu"] = (np.arange(128)[:, None] < np.arange(128)[None, :]).astype(np.float32)
    tb["t_ebase"] = np.tile((np.arange(NE) * CAP).astype(np.float32)[None, :], (128, 1))
    return tb
```
